# Optimizing a Trainium2 kernel written in Bass

```python
import math
import jax
import jax.numpy as jnp
from jax import lax
import numpy as np

D_MODEL = 1024
BATCH = 1
SEQ = 16384
DEPTH = 4

CTX_LEN = 256
GRID_W = 64

N_HEADS = 16
N_KV_HEADS = 4
HEAD_DIM = 64
GROUP = N_HEADS // N_KV_HEADS
WINDOW = 128
BLOCK = 128
ROPE_BASE = 10000.0
ROPE_AXIS_DIM = HEAD_DIM // 2

GLA_HEADS = 4
GLA_DK = (D_MODEL // 2) // GLA_HEADS
GLA_DV = D_MODEL // GLA_HEADS
GLA_RANK = 16
GLA_NORMALIZER = 16.0
GLA_CHUNK = 64

D_FF = 4 * D_MODEL
N_MOD = 6
EPS = 1e-6

ATT_Q = N_HEADS * HEAD_DIM
ATT_KV = N_KV_HEADS * HEAD_DIM
GLA_K = GLA_HEADS * GLA_DK
GLA_V = GLA_HEADS * GLA_DV
IN_SIZES = (ATT_Q, ATT_KV, ATT_KV, GLA_K, GLA_K, GLA_V, GLA_V, 2 * GLA_RANK, D_MODEL, D_MODEL)
D_IN = ATT_Q + 2 * ATT_KV + 2 * GLA_K + 2 * GLA_V + 2 * GLA_RANK + 2 * D_MODEL

kernel_name = "hybrid_gqa_gla_prefix_dit"

NEG = -1e30


def rms_norm(x, g):
    xf = x.astype(jnp.float32)
    y = xf * lax.rsqrt(jnp.mean(xf * xf, axis=-1, keepdims=True) + EPS)
    return (y * g.astype(jnp.float32)).astype(x.dtype)


def modulate(h, shift, scale):
    return h * (1.0 + scale) + shift


def split_in(z):
    out, idx = [], 0
    for s in IN_SIZES:
        out.append(z[..., idx:idx + s])
        idx += s
    return out


def rope_1d(x, pos):
    d = x.shape[-1]
    half = d // 2
    inv = ROPE_BASE ** (-jnp.arange(half, dtype=jnp.float32) * 2.0 / d)
    ang = pos.astype(jnp.float32)[:, None] * inv[None, :]
    cos = jnp.cos(ang)[None, :, None, :]
    sin = jnp.sin(ang)[None, :, None, :]
    xf = x.astype(jnp.float32)
    x1, x2 = xf[..., :half], xf[..., half:]
    return jnp.concatenate([x1 * cos - x2 * sin, x2 * cos + x1 * sin], axis=-1).astype(x.dtype)


def rope_2d(x, row, col):
    return jnp.concatenate([rope_1d(x[..., :ROPE_AXIS_DIM], row),
                            rope_1d(x[..., ROPE_AXIS_DIM:], col)], axis=-1)


def window_attention(q, k, v, kc, vc, sink):
    B, N = q.shape[0], q.shape[1]
    L = kc.shape[1]
    nb = N // BLOCK
    scale = HEAD_DIM ** -0.5
    qb = q.reshape(B, nb, BLOCK, N_KV_HEADS, GROUP, HEAD_DIM)
    pad = ((0, 0), (BLOCK, BLOCK), (0, 0), (0, 0))
    kp = jnp.pad(k, pad).reshape(B, nb + 2, BLOCK, N_KV_HEADS, HEAD_DIM)
    vp = jnp.pad(v, pad).reshape(B, nb + 2, BLOCK, N_KV_HEADS, HEAD_DIM)
    kw = jnp.concatenate([kp[:, :-2], kp[:, 1:-1], kp[:, 2:]], axis=2)
    vw = jnp.concatenate([vp[:, :-2], vp[:, 1:-1], vp[:, 2:]], axis=2)
    s_loc = jnp.einsum('bnqkgd,bnskd->bnkgqs', qb, kw).astype(jnp.float32) * scale
    qpos = jnp.arange(nb)[:, None] * BLOCK + jnp.arange(BLOCK)[None, :]
    kpos = (jnp.arange(nb)[:, None] - 1) * BLOCK + jnp.arange(3 * BLOCK)[None, :]
    valid = (jnp.abs(qpos[:, :, None] - kpos[:, None, :]) <= WINDOW) \
        & (kpos[:, None, :] >= 0) & (kpos[:, None, :] < N)
    s_loc = jnp.where(valid[None, :, None, None], s_loc, NEG)
    s_ctx = jnp.einsum('bnqkgd,bckd->bnkgqc', qb, kc).astype(jnp.float32) * scale
    s_sink = jnp.broadcast_to(sink.astype(jnp.float32).reshape(1, 1, N_KV_HEADS, GROUP, 1, 1),
                              (B, nb, N_KV_HEADS, GROUP, BLOCK, 1))
    p = jax.nn.softmax(jnp.concatenate([s_loc, s_ctx, s_sink], axis=-1), axis=-1)
    p_loc = p[..., :3 * BLOCK].astype(v.dtype)
    p_ctx = p[..., 3 * BLOCK:3 * BLOCK + L].astype(v.dtype)
    o = jnp.einsum('bnkgqs,bnskd->bnqkgd', p_loc, vw) + jnp.einsum('bnkgqc,bckd->bnqkgd', p_ctx, vc)
    return o.reshape(B, N, ATT_Q)


def context_attention(qc, kc, vc, sink):
    B, L = qc.shape[0], qc.shape[1]
    scale = HEAD_DIM ** -0.5
    qg = qc.reshape(B, L, N_KV_HEADS, GROUP, HEAD_DIM)
    s = jnp.einsum('bqkgd,bckd->bkgqc', qg, kc).astype(jnp.float32) * scale
    s_sink = jnp.broadcast_to(sink.astype(jnp.float32).reshape(1, N_KV_HEADS, GROUP, 1, 1),
                              (B, N_KV_HEADS, GROUP, L, 1))
    p = jax.nn.softmax(jnp.concatenate([s, s_sink], axis=-1), axis=-1)[..., :L].astype(vc.dtype)
    o = jnp.einsum('bkgqc,bckd->bqkgd', p, vc)
    return o.reshape(B, L, ATT_Q)


def gla_chunked(q, k, v, g, s0):
    B, H, N, dk = q.shape
    dv = v.shape[-1]
    C = GLA_CHUNK
    nc = N // C
    q = q.reshape(B, H, nc, C, dk)
    k = k.reshape(B, H, nc, C, dk)
    v = v.reshape(B, H, nc, C, dv)
    b = jnp.cumsum(g.reshape(B, H, nc, C, dk), axis=3)
    b_last = b[..., -1:, :]
    q_in = q * jnp.exp(b)
    k_in = k * jnp.exp(-b)
    k_out = k * jnp.exp(b_last - b)
    a = jnp.einsum('bhnid,bhnjd->bhnij', q_in, k_in)
    a = jnp.where(jnp.tril(jnp.ones((C, C), dtype=bool)), a, 0.0)
    o_intra = jnp.einsum('bhnij,bhnje->bhnie', a, v)

    def step(S, inp):
        qi, ko, vi, dl = inp
        o = jnp.einsum('bhid,bhde->bhie', qi, S)
        S = S * dl[..., None] + jnp.einsum('bhjd,bhje->bhde', ko, vi)
        return S, o

    xs = (jnp.moveaxis(q_in, 2, 0), jnp.moveaxis(k_out, 2, 0), jnp.moveaxis(v, 2, 0),
          jnp.moveaxis(jnp.exp(b_last[..., 0, :]), 2, 0))
    s_fin, o_inter = lax.scan(step, s0, xs)
    o = o_intra + jnp.moveaxis(o_inter, 0, 2)
    return o.reshape(B, H, N, dv), s_fin


def gla_inputs(gq, gk, gv, ga, w_decay, b_decay):
    B, N = gq.shape[0], gq.shape[1]

    def heads(t, d):
        return t.astype(jnp.float32).reshape(B, N, GLA_HEADS, d).transpose(0, 2, 1, 3)

    q = heads(gq, GLA_DK) * (GLA_DK ** -0.5)
    k = heads(gk, GLA_DK)
    v = heads(gv, GLA_DV)
    g_f = jax.nn.log_sigmoid((ga[..., :GLA_RANK] @ w_decay[0] + b_decay[0]).astype(jnp.float32)) / GLA_NORMALIZER
    g_b = jax.nn.log_sigmoid((ga[..., GLA_RANK:] @ w_decay[1] + b_decay[1]).astype(jnp.float32)) / GLA_NORMALIZER
    return q, k, v, heads(g_f, GLA_DK), heads(g_b, GLA_DK)


def bidir_gla(q, k, v, g_f, g_b, s0_f, s0_b):
    o_f, s_f = gla_chunked(q, k, v, g_f, s0_f)
    flip = lambda t: jnp.flip(t, axis=2)
    o_b, s_b = gla_chunked(flip(q), flip(k), flip(v), flip(g_b), s0_b)
    return o_f + flip(o_b), s_f, s_b


def gla_output(o, gr, gain):
    B, H, N, dv = o.shape
    o = rms_norm(o.transpose(0, 2, 1, 3), gain).reshape(B, N, H * dv)
    return o.astype(gr.dtype) * jax.nn.silu(gr)


def squared_relu_mlp(h, w1, w2):
    return jnp.square(jax.nn.relu(h @ w1)) @ w2


def setup_inputs(seed: int = 0) -> dict:
    key = jax.random.key(seed)
    ks = jax.random.split(key, 20)
    nrm = lambda k, shape, s: jax.random.normal(k, shape, dtype=jnp.float32) * s
    D = D_MODEL
    return {
        "x": nrm(ks[0], (BATCH, SEQ, D), 1.0),
        "c": nrm(ks[1], (BATCH, D), 1.0),
        "ctx": nrm(ks[2], (BATCH, CTX_LEN, D), 1.0),
        "c_ctx": nrm(ks[3], (D,), 1.0),
        "w_mod": nrm(ks[4], (DEPTH, D, N_MOD * D), 0.5 * D ** -0.5),
        "b_mod": nrm(ks[5], (DEPTH, N_MOD * D), 0.02),
        "g_norm1": 1.0 + nrm(ks[6], (DEPTH, D), 0.1),
        "w_in": nrm(ks[7], (DEPTH, D, D_IN), D ** -0.5),
        "q_gain": 1.0 + nrm(ks[8], (DEPTH, HEAD_DIM), 0.1),
        "k_gain": 1.0 + nrm(ks[9], (DEPTH, HEAD_DIM), 0.1),
        "sink": nrm(ks[10], (DEPTH, N_HEADS), 0.5),
        "w_decay": nrm(ks[11], (DEPTH, 2, GLA_RANK, GLA_K), GLA_RANK ** -0.5),
        "b_decay": nrm(ks[12], (DEPTH, 2, GLA_K), 0.5),
        "gla_gain": 1.0 + nrm(ks[13], (DEPTH, GLA_DV), 0.1),
        "w_branch_attn": nrm(ks[14], (DEPTH, ATT_Q, D), ATT_Q ** -0.5),
        "w_branch_gla": nrm(ks[15], (DEPTH, GLA_V, D), GLA_V ** -0.5),
        "w_out": nrm(ks[16], (DEPTH, D, D), D ** -0.5),
        "g_norm2": 1.0 + nrm(ks[17], (DEPTH, D), 0.1),
        "w_ff1": nrm(ks[18], (DEPTH, D, D_FF), D ** -0.5),
        "w_ff2": nrm(ks[19], (DEPTH, D_FF, D), D_FF ** -0.5),
    }


def reference(x, c, ctx, c_ctx, w_mod, b_mod, g_norm1, w_in, q_gain, k_gain, sink, w_decay, b_decay,
              gla_gain, w_branch_attn, w_branch_gla, w_out, g_norm2, w_ff1, w_ff2):
    B, N = x.shape[0], x.shape[1]
    L = ctx.shape[1]
    ROWS = N // GRID_W
    row = jnp.repeat(jnp.arange(ROWS, dtype=jnp.int32), GRID_W)
    col = jnp.tile(jnp.arange(GRID_W, dtype=jnp.int32), ROWS)
    silu_c = jax.nn.silu(c)
    silu_cc = jax.nn.silu(c_ctx)
    xc = ctx
    s_zero = jnp.zeros((B, GLA_HEADS, GLA_DK, GLA_DV), jnp.float32)

    for l in range(DEPTH):
        last = l == DEPTH - 1
        mod = (silu_c @ w_mod[l] + b_mod[l])[:, None, :]
        mod_c = (silu_cc @ w_mod[l] + b_mod[l])[None, None, :]
        sh1, sc1, gt1, sh2, sc2, gt2 = jnp.split(mod, N_MOD, axis=-1)
        sh1c, sc1c, gt1c, sh2c, sc2c, gt2c = jnp.split(mod_c, N_MOD, axis=-1)

        h = modulate(rms_norm(x, g_norm1[l]), sh1, sc1)
        hc = modulate(rms_norm(xc, g_norm1[l]), sh1c, sc1c)
        aq, ak, av, gq, gk, gv, gr, ga, gate_a, gate_b = split_in(h @ w_in[l])
        aqc, akc, avc, gqc, gkc, gvc, grc, gac, gate_ac, gate_bc = split_in(hc @ w_in[l])

        q = rope_2d(rms_norm(aq.reshape(B, N, N_HEADS, HEAD_DIM), q_gain[l]), row, col)
        k = rope_2d(rms_norm(ak.reshape(B, N, N_KV_HEADS, HEAD_DIM), k_gain[l]), row, col)
        v = av.reshape(B, N, N_KV_HEADS, HEAD_DIM)
        kc = rms_norm(akc.reshape(B, L, N_KV_HEADS, HEAD_DIM), k_gain[l])
        vc = avc.reshape(B, L, N_KV_HEADS, HEAD_DIM)
        attn = window_attention(q, k, v, kc, vc, sink[l])

        qg_c, kg_c, vg_c, gf_c, gb_c = gla_inputs(gqc, gkc, gvc, gac, w_decay[l], b_decay[l])
        o_c, s_f, s_b = bidir_gla(qg_c, kg_c, vg_c, gf_c, gb_c, s_zero, s_zero)
        qg, kg, vg, gf, gb = gla_inputs(gq, gk, gv, ga, w_decay[l], b_decay[l])
        o_l, _, _ = bidir_gla(qg, kg, vg, gf, gb, s_f, s_b)
        gla = gla_output(o_l, gr, gla_gain[l])

        y = jax.nn.sigmoid(gate_a) * (attn @ w_branch_attn[l]) + jax.nn.sigmoid(gate_b) * (gla @ w_branch_gla[l])
        x = x + gt1 * (y @ w_out[l])

        x = x + gt2 * squared_relu_mlp(modulate(rms_norm(x, g_norm2[l]), sh2, sc2), w_ff1[l], w_ff2[l])

        if not last:
            qc = rms_norm(aqc.reshape(B, L, N_HEADS, HEAD_DIM), q_gain[l])
            attn_c = context_attention(qc, kc, vc, sink[l])
            gla_c = gla_output(o_c, grc, gla_gain[l])
            yc = jax.nn.sigmoid(gate_ac) * (attn_c @ w_branch_attn[l]) \
                + jax.nn.sigmoid(gate_bc) * (gla_c @ w_branch_gla[l])
            xc = xc + gt1c * (yc @ w_out[l])
            xc = xc + gt2c * squared_relu_mlp(modulate(rms_norm(xc, g_norm2[l]), sh2c, sc2c), w_ff1[l], w_ff2[l])

    return x
```

```python
import numpy as np
from contextlib import ExitStack
import concourse.bass as bass
import concourse.mybir as mybir
from concourse.bass_utils import run_bass_kernel_spmd

F32 = mybir.dt.float32
BF16 = mybir.dt.bfloat16
AF = mybir.ActivationFunctionType
ALU = mybir.AluOpType
AX = mybir.AxisListType

D = 1024
DIN = 6688
DFF = 4096
NH = 16
NKV = 4
HD = 64
EPS = 1e-6
FS = 2824
AW = 52000

ENGS = ['tensor', 'vector', 'scalar', 'gpsimd', 'sync']
DMAQ = ['sync', 'gpsimd']
RING = 8


class Dep:
    __slots__ = ('name', 'w', 'r')

    def __init__(self, name=''):
        self.name = name
        self.w = None
        self.r = {}


class FW:
    def __init__(self, nc, stack):
        self.nc = nc
        self.ops = {e: [] for e in ENGS}
        self.seq = {e: 0 for e in ENGS}
        self.known = {e: {} for e in ENGS}
        self.sems = {}
        for e in ENGS:
            self.sems[e] = stack.enter_context(nc.semaphore('s_' + e))
        self.ring_pos = {q: 0 for q in DMAQ}
        self.ring_cnt = {}
        for q in DMAQ:
            for i in range(RING):
                k = ('dma', q, i)
                self.sems[k] = stack.enter_context(nc.semaphore('d_%s_%d' % (q, i)))
                self.ring_cnt[k] = 0
        self.sems['cc'] = stack.enter_context(nc.semaphore('s_cc'))
        self.cc_cnt = 0
        self.nwaits = 0
        self.nops = 0

    def _need(self, reads, writes):
        need = {}
        for d in reads:
            if d.w is not None and need.get(d.w[0], 0) < d.w[1]:
                need[d.w[0]] = d.w[1]
        for d in writes:
            if d.w is not None and need.get(d.w[0], 0) < d.w[1]:
                need[d.w[0]] = d.w[1]
            for s, v in d.r.items():
                if need.get(s, 0) < v:
                    need[s] = v
        return need

    def _emit_waits(self, eng, need, own=False):
        kn = self.known[eng]
        for s, v in need.items():
            if s == eng and eng == 'tensor' and not own:
                continue
            if kn.get(s, 0) >= v:
                continue
            kn[s] = v
            self.ops[eng].append(('wait', s, v))
            self.nwaits += 1

    def _commit(self, ev, reads, writes):
        s, v = ev
        for d in writes:
            d.w = ev
            d.r = {}
        for d in reads:
            if d.r.get(s, 0) < v:
                d.r[s] = v

    def op(self, eng, fn, reads=(), writes=()):
        self._emit_waits(eng, self._need(reads, writes))
        self.seq[eng] += 1
        self.nops += 1
        ev = (eng, self.seq[eng])
        self.ops[eng].append(('op', fn, eng, 1))
        self._commit(ev, reads, writes)

    def dma(self, q, out, in_, reads=(), writes=()):
        need = self._need(reads, writes)
        i = self.ring_pos[q]
        self.ring_pos[q] = (i + 1) % RING
        k = ('dma', q, i)
        if self.ring_cnt[k] > 0 and need.get(k, 0) < self.ring_cnt[k]:
            need[k] = self.ring_cnt[k]
        self._emit_waits(q, need)
        self.ring_cnt[k] += 16
        self.nops += 1
        ev = (k, self.ring_cnt[k])
        self.ops[q].append(('op', lambda e, o=out, i_=in_: e.dma_start(out=o, in_=i_), k, 16))
        self._commit(ev, reads, writes)

    def coll(self, fn, reads=(), writes=()):
        self._emit_waits('gpsimd', self._need(reads, writes))
        self.cc_cnt += 1
        self.nops += 1
        ev = ('cc', self.cc_cnt)
        self.ops['gpsimd'].append(('op', fn, 'cc', 1))
        self._commit(ev, reads, writes)

    def barrier(self):
        for e in ENGS:
            need = {}
            for o in ENGS:
                if self.seq[o] > 0 and not (o == e and e == 'sync'):
                    need[o] = self.seq[o]
            if self.cc_cnt > 0:
                need['cc'] = self.cc_cnt
            for k, c in self.ring_cnt.items():
                if c > 0:
                    need[k] = c
            self._emit_waits(e, need, own=True)
        for e in ENGS:
            self.ops[e].append(('seg',))

    def emit(self):
        nc = self.nc
        segs = {e: [[]] for e in ENGS}
        for e in ENGS:
            for o in self.ops[e]:
                if o[0] == 'seg':
                    segs[e].append([])
                else:
                    segs[e][-1].append(o)
        nseg = max(len(segs[e]) for e in ENGS)
        for si in range(nseg):
            if not any(si < len(segs[e]) and segs[e][si] for e in ENGS):
                continue
            with nc.Block() as block:
                for e in ENGS:
                    ops = segs[e][si] if si < len(segs[e]) else []
                    if not ops:
                        continue

                    def body(eng, ops=ops):
                        for o in ops:
                            if o[0] == 'wait':
                                eng.wait_ge(self.sems[o[1]], o[2])
                            else:
                                o[1](eng).then_inc(self.sems[o[2]], o[3])
                    getattr(block, e)(body)


class Buf:
    __slots__ = ('ap', 'd')

    def __init__(self, ap, d):
        self.ap = ap
        self.d = d

    def __getitem__(self, key):
        return self.ap[key]


def _prod(xs):
    r = 1
    for x in xs:
        r *= x
    return r


class K:
    def __init__(self, nc, st, NC, NT, DEPTH):
        self.nc = nc
        self.st = st
        self.NC, self.NT, self.DEPTH = NC, NT, DEPTH
        self.T = NT + 2
        self.fw = FW(nc, st)
        self.arena = st.enter_context(nc.sbuf_tensor("arena", [128, AW], F32))
        self.top = 0
        self.banks = []
        for i in range(8):
            pt = st.enter_context(nc.psum_tensor("pb%d" % i, [128, 512], F32))
            self.banks.append(Buf(pt[:, :], Dep('pb%d' % i)))
        self.busy = [False] * 8
        self.bnext = 0
        self.rots = {}

    def buf(self, name, free, dt):
        n = _prod(free)
        words = n if dt == F32 else (n + 1) // 2
        words = (words + 7) // 8 * 8
        off = self.top
        self.top += words
        assert self.top <= AW, "arena overflow at %s: %d" % (name, self.top)
        ap = self.arena[:, off:off + words]
        if dt != F32:
            ap = ap.bitcast(dt)
        ap = ap[:, 0:n]
        if len(free) == 2:
            ap = ap.rearrange("p (a b) -> p a b", b=free[1])
        elif len(free) == 3:
            ap = ap.rearrange("p (a b c) -> p a b c", b=free[1], c=free[2])
        elif len(free) == 4:
            ap = ap.rearrange("p (a b c e) -> p a b c e", b=free[1], c=free[2], e=free[3])
        return Buf(ap, Dep(name))

    def rot(self, name, n, free, dt):
        bs = [self.buf("%s%d" % (name, i), free, dt) for i in range(n)]
        self.rots[name] = [bs, 0]
        return bs

    def nxt(self, name):
        r = self.rots[name]
        b = r[0][r[1] % len(r[0])]
        r[1] += 1
        return b

    def mark(self):
        return self.top

    def release_to(self, m):
        self.top = m

    def psum(self):
        for j in range(8):
            i = (self.bnext + j) % 8
            if not self.busy[i]:
                self.busy[i] = True
                self.bnext = (i + 1) % 8
                return i, self.banks[i]
        raise RuntimeError("all PSUM banks busy")

    def free(self, i):
        self.busy[i] = False

    def act(self, out, in_, func, R, W, scale=1.0, bias=None, accum=None):
        kw = dict(out=out, in_=in_, func=func, scale=scale)
        if bias is not None:
            kw['bias'] = bias
        if accum is not None:
            kw['accum_out'] = accum
        self.fw.op('scalar', lambda e: e.activation(**kw), R, W)

    def tt(self, eng, out, in0, in1, op, R, W):
        self.fw.op(eng, lambda e: e.tensor_tensor(out=out, in0=in0, in1=in1, op=op), R, W)

    def ts(self, eng, out, in0, s1, op0, R, W, s2=None, op1=None):
        if op1 is None:
            self.fw.op(eng, lambda e: e.tensor_scalar(out=out, in0=in0, scalar1=s1, scalar2=None, op0=op0), R, W)
        else:
            self.fw.op(eng, lambda e: e.tensor_scalar(out=out, in0=in0, scalar1=s1, scalar2=s2, op0=op0, op1=op1), R, W)

    def stt(self, out, in0, scalar, in1, op0, op1, R, W):
        self.fw.op('vector', lambda e: e.scalar_tensor_tensor(out=out, in0=in0, scalar=scalar, in1=in1,
                                                              op0=op0, op1=op1), R, W)

    def cp(self, eng, out, in_, R, W):
        if eng == 'scalar':
            self.fw.op('scalar', lambda e: e.copy(out=out, in_=in_), R, W)
        else:
            self.fw.op(eng, lambda e: e.tensor_copy(out=out, in_=in_), R, W)

    def red(self, out, in_, R, W):
        self.fw.op('vector', lambda e: e.tensor_reduce(out=out, in_=in_, axis=AX.X, op=ALU.add), R, W)

    def recip(self, out, in_, R, W):
        self.fw.op('vector', lambda e: e.reciprocal(out=out, in_=in_), R, W)

    def memset(self, eng, ap, val, W):
        self.fw.op(eng, lambda e: e.memset(ap, val), (), W)

    def mm(self, out, lhsT, rhs, start, stop, R, W):
        self.fw.op('tensor', lambda e: e.matmul(out, lhsT=lhsT, rhs=rhs, start=start, stop=stop), R, W)

    def tr(self, out, in_, ident, R, W):
        self.fw.op('tensor', lambda e: e.transpose(out=out, in_=in_, identity=ident), R, W)

    def dma(self, q, out, in_, R, W):
        self.fw.dma(q, out, in_, R, W)

    def rstd(self, out, ss, n, R_, W_, tmp):
        self.act(tmp, ss, AF.Ln, list(R_) + [self.epsb.d], W_, scale=1.0 / n, bias=self.epsb[:, 0:1])
        self.act(out, tmp, AF.Exp, W_, W_, scale=-0.5)


def build_program(NC, NT, DEPTH, dbg=False):
    T = NT + 2
    NS = NT + 4
    WM = D // NC
    MODW = DEPTH * 6 * WM
    nc = bass.Bass("TRN2", target_bir_lowering=False)
    dt_in = lambda name, shape: nc.dram_tensor(name, shape, F32, kind="ExternalInput").ap()
    scr_kind = "ExternalOutput" if dbg else "Internal"

    def dscr(name, shape, dt=F32):
        if dbg:
            return nc.dram_tensor(name, shape, dt, kind="ExternalOutput")
        return nc.dram_tensor(name, shape, dt)

    x_in = dt_in("x", [NT * 128, D])
    ctx_in = dt_in("ctx", [256, D])
    cvecT = dt_in("cvecT", [128, 8, 2])
    meta = dt_in("meta", [128, 2 * T + 4 * NC + 2])
    w_mod = dt_in("w_mod_s", [DEPTH, D, 6 * WM])
    b_mod = dt_in("b_mod_s", [1, MODW])
    g_norm1 = dt_in("g_norm1", [DEPTH, D])
    g_norm2 = dt_in("g_norm2", [DEPTH, D])
    w_in = dt_in("w_in", [DEPTH, D, DIN])
    q_gain = dt_in("q_gain", [DEPTH, HD])
    k_gain = dt_in("k_gain", [DEPTH, HD])
    sink = dt_in("sink", [DEPTH, NH])
    w_decay = dt_in("w_decay", [DEPTH, 2, 16, 512])
    b_decay = dt_in("b_decay", [DEPTH, 2, 512])
    gla_gain = dt_in("gla_gain", [DEPTH, 256])
    w_ba = dt_in("w_branch_attn", [DEPTH, D, D])
    w_bg = dt_in("w_branch_gla", [DEPTH, D, D])
    w_out = dt_in("w_out", [DEPTH, D, D])
    w_ff1 = dt_in("w_ff1", [DEPTH, D, DFF])
    w_ff2 = dt_in("w_ff2", [DEPTH, DFF, D])
    out = nc.dram_tensor("out", [NT * 128, D], F32, kind="ExternalOutput").ap()

    XC = dscr("XC", [T * 128, D]).ap()
    XM = dscr("XM", [T * 128, D]).ap()
    MS = nc.dram_tensor("MS", [2, MODW], F32)
    MG = nc.dram_tensor("MG", [NC * 2, MODW], F32)
    XS = [nc.dram_tensor("XS%d" % l, [128, FS], F32) for l in range(DEPTH)]
    XG = [nc.dram_tensor("XG%d" % l, [NC * 128, FS], F32) for l in range(DEPTH)]
    SC_hT = dscr("SC_hT", [T, 128, 1024], BF16).ap()
    SC_qT = dscr("SC_qT", [T, 64, 2048], BF16).ap()
    SC_qin = dscr("SC_qin", [T, 2, 128, 512], BF16).ap()
    SC_kin = dscr("SC_kin", [T, 2, 128, 512], BF16).ap()
    SC_kout = dscr("SC_kout", [T, 2, 128, 512], BF16).ap()
    SC_gv = dscr("SC_gv", [T, 128, 1024], BF16).ap()
    SC_gr = dscr("SC_gr", [T, 128, 1024], BF16).ap()
    SC_ga = dscr("SC_ga", [T, 128, 1024], BF16).ap()
    SC_gb = dscr("SC_gb", [T, 128, 1024], BF16).ap()
    ddeps = {}

    def dd(*key):
        if key not in ddeps:
            ddeps[key] = Dep(str(key))
        return ddeps[key]

    with ExitStack() as st:
        k = K(nc, st, NC, NT, DEPTH)
        fw = k.fw
        P = k.banks

        ident = k.buf("ident", [128], BF16)
        Uf = k.buf("Uf", [128], F32)
        Ub = k.buf("Ub", [128], F32)
        SUf = k.buf("SUf", [128], F32)
        SUb = k.buf("SUb", [128], F32)
        maskL = k.buf("maskL", [128], BF16)
        maskR = k.buf("maskR", [128], BF16)
        ones_b = k.buf("ones_b", [64], BF16)
        ones_f = k.buf("ones_f", [128], F32)
        k.epsb = k.buf("epsb", [1], F32)
        metab = k.buf("metab", [2 * T + 4 * NC + 2], F32)
        cosT = k.buf("cosT", [T, 2, 16], F32)
        sinT = k.buf("sinT", [T, 2, 16], F32)
        c_pos = 0
        c_mf = 2 * T
        c_mb = c_mf + NC
        c_sl = c_mb + NC
        c_sr = c_sl + NC
        c_el = c_sr + NC
        c_er = c_el + 1

        k.memset('gpsimd', k.epsb[:, :], EPS, [k.epsb.d])
        k.memset('gpsimd', ones_b[:, :], 1.0, [ones_b.d])
        k.memset('gpsimd', ones_f[:, :], 1.0, [ones_f.d])

        def tri(b, cmp, base, cm, pat):
            k.memset('gpsimd', b[:, :], 1.0, [b.d])
            fw.op('gpsimd', lambda e: e.affine_select(out=b[:, :], in_=b[:, :], pattern=[[pat, 128]], base=base,
                                                      channel_multiplier=cm, compare_op=cmp, fill=0.0),
                  [b.d], [b.d])
        k.memset('gpsimd', ident[:, :], 0.0, [ident.d])
        fw.op('gpsimd', lambda e: e.affine_select(out=ident[:, :], in_=ident[:, :], pattern=[[-1, 128]], base=0,
                                                  channel_multiplier=1, compare_op=ALU.not_equal, fill=1.0),
              [ident.d], [ident.d])
        tri(Uf, ALU.is_ge, 0, -1, 1)
        tri(Ub, ALU.is_ge, 0, 1, -1)
        tri(SUf, ALU.is_gt, 0, 1, -1)
        tri(SUb, ALU.is_gt, 0, -1, 1)
        k.dma('sync', metab[:, :], meta[:, :], [], [metab.d])
        k.ts('vector', maskL[:, :], Ub[:, :], metab[:, c_el:c_el + 1], ALU.mult, [Ub.d, metab.d], [maskL.d])
        k.ts('vector', maskR[:, :], Uf[:, :], metab[:, c_er:c_er + 1], ALU.mult, [Uf.d, metab.d], [maskR.d])
        triL = k.buf("triL", [128], BF16)
        triR = k.buf("triR", [128], BF16)
        k.cp('vector', triL[:, :], Ub[:, :], [Ub.d], [triL.d])
        k.cp('vector', triR[:, :], Uf[:, :], [Uf.d], [triR.d])

        m0 = k.mark()
        invf = k.buf("invf", [16], F32)
        iot = k.buf("iot", [16], F32)
        yv = k.buf("yv", [T, 2, 16], F32)
        kv = k.buf("kv", [T, 2, 16], F32)
        fw.op('gpsimd', lambda e: e.iota(iot[:, :], [[1, 16]], base=0, channel_multiplier=0,
                                         allow_small_or_imprecise_dtypes=True), [], [iot.d])
        lb = k.buf("lb", [1], F32)
        k.memset('gpsimd', lb[:, :], -float(np.log(2 * np.pi)), [lb.d])
        k.act(invf[:, :], iot[:, :], AF.Exp, [iot.d, lb.d], [invf.d], scale=-float(np.log(10000.0)) / 16.0,
              bias=lb[:, 0:1])
        posv = metab[:, 0:2 * T].rearrange("p (t a) -> p t a", a=2)
        k.tt('vector', yv[:, :, :, :], posv.unsqueeze(3).broadcast_to([128, T, 2, 16]),
             invf[:, :].unsqueeze(1).unsqueeze(1).broadcast_to([128, T, 2, 16]), ALU.mult, [metab.d, invf.d], [yv.d])
        MAGIC = 12582912.0
        SC2 = float(2 * np.pi * (1 - 1e-6))
        for (dst, shift) in ((sinT, 0.0), (cosT, 0.25)):
            if shift != 0.0:
                k.ts('vector', yv[:, :, :, :], yv[:, :, :, :], shift, ALU.add, [yv.d], [yv.d])
            k.ts('vector', kv[:, :, :, :], yv[:, :, :, :], MAGIC, ALU.add, [yv.d], [kv.d])
            k.ts('vector', kv[:, :, :, :], kv[:, :, :, :], MAGIC, ALU.subtract, [kv.d], [kv.d])
            k.tt('vector', kv[:, :, :, :], yv[:, :, :, :], kv[:, :, :, :], ALU.subtract, [yv.d, kv.d], [kv.d])
            k.act(dst[:, :, :, :], kv[:, :, :, :], AF.Sin, [kv.d], [dst.d], scale=SC2)
        fw.barrier()
        k.release_to(m0)

        m0 = k.mark()
        cT = k.buf("cT", [8, 2], F32)
        scT = k.buf("scT", [8, 2], F32)
        wm = k.rot("wm", 2, [8, 512], F32)
        bm = k.buf("bm", [MODW], F32)
        modsb = k.buf("modsb", [MODW], F32)
        k.dma('sync', cT[:, :, :], cvecT[:, :, :], [], [cT.d])
        k.dma('sync', bm[0:1, :], b_mod[0:1, :], [], [bm.d])
        k.act(scT[:, :, :], cT[:, :, :], AF.Silu, [cT.d], [scT.d])
        for l in range(DEPTH):
            ncol = 6 * WM
            c0 = 0
            while c0 < ncol:
                cw = min(512, ncol - c0)
                wt = k.nxt("wm")
                k.dma('sync', wt[:, :, 0:cw], w_mod[l].rearrange("(kc p) n -> p kc n", p=128)[:, :, c0:c0 + cw], [], [wt.d])
                bi, pb = k.psum()
                for kc in range(8):
                    k.mm(pb[0:2, 0:cw], scT[:, kc, :], wt[:, kc, 0:cw], kc == 0, False, [scT.d, wt.d], [pb.d])
                k.mm(pb[0:2, 0:cw], ones_f[0:1, 0:2], bm[0:1, l * ncol + c0:l * ncol + c0 + cw], False, True,
                     [ones_f.d, bm.d], [pb.d])
                k.cp('vector', modsb[0:2, l * ncol + c0:l * ncol + c0 + cw], pb[0:2, 0:cw], [pb.d], [modsb.d])
                k.free(bi)
                c0 += cw
        dMS, dMG = Dep('MS'), Dep('MG')
        k.dma('sync', MS.ap()[:, :], modsb[0:2, :], [modsb.d], [dMS])
        if NC == 1:
            k.dma('sync', MG.ap()[:, :], MS.ap()[:, :], [dMS], [dMG])
        else:
            fw.coll(lambda e: e.collective_compute("AllGather", ALU.bypass, replica_groups=[list(range(NC))],
                                                   ins=[MS.ap().opt()], outs=[MG.ap().opt()]), [dMS], [dMG])
        fw.barrier()
        k.release_to(m0)

        def mod_bcast(dst, l, m, j):
            col0 = (l * 6 + m) * WM
            src = MG.ap()[j::2, col0:col0 + WM].partition_broadcast(128)
            k.dma('sync', dst[:, :].rearrange("p (r w) -> p r w", w=WM), src, [dMG], [dst.d])

        def vec_bcast(dst, src_row, n, parts=128):
            k.dma('sync', dst, src_row.partition_broadcast(parts), [], [])

        def xsrc(l, t):
            if l == 0:
                if t < NT:
                    return x_in[t * 128:(t + 1) * 128, :], None
                return ctx_in[(t - NT) * 128:(t - NT + 1) * 128, :], None
            return XC[t * 128:(t + 1) * 128, :], dd('XC', t)

        def load_w(dst, src, dep):
            k.dma('gpsimd', dst, src, [], [dep])

        SC_aT = dscr("SC_aT", [T, 64, 2048], BF16).ap()
        SC_gT = dscr("SC_gT", [T, 128, 1024], BF16).ap()
        SC_SB = dscr("SC_SB", [T, 128, 1024], BF16).ap()
        for l in range(DEPTH):
            last = (l == DEPTH - 1)
            tiles_all = [NT, NT + 1] + list(range(NT))
            tiles_out = list(range(NT)) if last else list(range(NT)) + [NT, NT + 1]
            mL = k.mark()
            KT = k.buf("KT", [NKV, NS * 128], BF16)
            VA = k.buf("VA", [NS, NKV, HD], BF16)
            KTd = [Dep('KT%d' % s) for s in range(NS)]
            VAd = [Dep('VA%d' % s) for s in range(NS)]
            DL = k.buf("DL", [T, 2, 4], F32)
            DLd = [Dep('DL%d' % t) for t in range(T)]
            Af = k.buf("Af", [4, 256], F32)
            Ab = k.buf("Ab", [4, 256], F32)
            Scf = k.buf("Scf", [4, 256], F32)
            Scb = k.buf("Scb", [4, 256], F32)
            PP = k.buf("PP", [4, 4], F32)
            for b_ in (Af, Ab, Scf, Scb):
                k.memset('gpsimd', b_[:, :, :], 0.0, [b_.d])
            k.memset('gpsimd', PP[:, :, :], 1.0, [PP.d])
            slot = lambda t: (t + 1) if t < NT else (t + 2)

            mA = k.mark()
            NCA = 3616
            W1 = k.buf("W1", [8, NCA], BF16)
            W1d = W1.d
            wv = w_in[l].rearrange("(kc p) n -> p kc n", p=128)
            for kc in range(8):
                load_w(W1[:, kc, 0:3584], wv[:, kc, 0:3584], W1d)
            load_w(W1[:, :, 3584:3616], wv[:, :, 4608:4640], W1d)
            WD = k.buf("WD", [2, 512], F32)
            k.dma('sync', WD[0:16, :, :], w_decay[l].rearrange("r p n -> p r n"), [], [WD.d])
            k.dma('sync', WD[16:17, :, :], b_decay[l:l + 1, :, :], [], [WD.d])
            G1 = k.buf("G1", [D], F32)
            SH1 = k.buf("SH1", [D], F32)
            sq = k.buf("sq", [D], F32)
            qn = k.buf("qn", [D], F32)

            def load_mod1(j):
                k.dma('sync', sq[:, :], g_norm1[l:l + 1, :].partition_broadcast(128), [], [sq.d])
                mod_bcast(SH1, l, 0, j)
                mod_bcast(G1, l, 1, j)
                k.stt(G1[:, :], G1[:, :], 1.0, sq[:, :], ALU.add, ALU.mult, [G1.d, sq.d], [G1.d])
            Gq = k.buf("Gq", [2, 2, 16], F32)
            GqS = k.buf("GqS", [2, 2, 16], F32)
            Gk = k.buf("Gk", [2, 2, 16], F32)
            GkS = k.buf("GkS", [2, 2, 16], F32)
            for (g_, gs_, src, sc) in ((Gq, GqS, q_gain, HD ** -0.5), (Gk, GkS, k_gain, 1.0)):
                k.dma('sync', g_[:, :, :, :].rearrange("p a f i -> p (a f i)"),
                      src[l:l + 1, :].partition_broadcast(128), [], [g_.d])
                if sc != 1.0:
                    k.ts('vector', g_[:, :, :, :], g_[:, :, :, :], sc, ALU.mult, [g_.d], [g_.d])
                k.ts('vector', gs_[:, :, 0, :], g_[:, :, 1, :], -1.0, ALU.mult, [g_.d], [gs_.d])
                k.cp('vector', gs_[:, :, 1, :], g_[:, :, 0, :], [g_.d], [gs_.d])
            k.rot("xt", 2, [D], F32)
            tmpx = qn
            h = k.buf("h", [D], BF16)
            k.rot("hT", 2, [8, 128], BF16)
            t1 = k.buf("t1", [512], F32)
            t2 = k.buf("t2", [512], F32)
            qrot = k.buf("qrot", [D], BF16)
            qT = k.buf("qT", [2048], BF16)
            k.rot("sm", 2, [128], F32)
            k.rot("gct", 2, [4, 64], F32)
            kn = k.buf("kn", [256], F32)
            krot = k.buf("krot", [256], BF16)
            k.rot("gp2", 2, [2, 512], F32)
            k.rot("E1", 2, [2, 512], BF16)
            k.rot("E2", 2, [2, 512], BF16)
            k.rot("ER", 2, [2, 512], BF16)
            k.rot("gv", 2, [D], BF16)
            gaT = k.rot("gaT", 2, [256], F32)
            for g_ in gaT:
                k.memset('gpsimd', g_[0:32, :], 1.0, [g_.d])
            k.rot("qin", 2, [2, 512], BF16)
            k.rot("kin", 2, [2, 512], BF16)
            k.rot("kout", 2, [2, 512], BF16)
            hb = k.buf("halo", [1536], BF16)
            k.memset('gpsimd', hb[:, :], 0.0, [hb.d])

            for ti, t in enumerate(tiles_all):
                isctx = t >= NT
                if ti == 0:
                    load_mod1(1)
                elif ti == 2:
                    load_mod1(0)
                sl = slot(t)
                xs_ap, xs_d = xsrc(l, t)
                xt = k.nxt("xt")
                k.dma('sync', xt[:, :], xs_ap, [xs_d] if xs_d else [], [xt.d])
                sm = k.nxt("sm")
                k.act(sq[:, :], xt[:, :], AF.Square, [xt.d], [sq.d])
                k.red(sm[:, 60:61], sq[:, :], [sq.d], [sm.d])
                k.rstd(sm[:, 61:62], sm[:, 60:61], D, [sm.d], [sm.d], sm[:, 62:63])
                k.tt('gpsimd', tmpx[:, :], xt[:, :], G1[:, :], ALU.mult, [xt.d, G1.d], [tmpx.d])
                k.stt(h[:, :], tmpx[:, :], sm[:, 61:62], SH1[:, :], ALU.mult, ALU.add, [tmpx.d, sm.d, SH1.d], [h.d])
                bi, pb = k.psum()
                pbv = pb.ap.bitcast(BF16)
                for kc in range(8):
                    k.tr(pbv[:, kc * 128:(kc + 1) * 128], h[:, kc * 128:(kc + 1) * 128], ident[:, :], [h.d, ident.d], [pb.d])
                hT = k.nxt("hT")
                k.cp('vector', hT[:, :, :].rearrange("p a b -> p (a b)"), pbv[:, 0:1024], [pb.d], [hT.d])
                k.free(bi)
                if t in tiles_out:
                    k.dma('gpsimd', SC_hT[t], hT[:, :, :].rearrange("p a b -> p (a b)"), [hT.d], [dd('hT', t)])

                def proj_tok(c0, ncols=512):
                    bi_, pb_ = k.psum()
                    for kc in range(8):
                        k.mm(pb_[:, 0:ncols], hT[:, kc, :], W1[:, kc, c0:c0 + ncols], kc == 0, kc == 7,
                             [hT.d, W1d], [pb_.d])
                    return bi_, pb_

                def proj_feat(c0):
                    bi_, pb_ = k.psum()
                    for hh in range(4):
                        for kc in range(8):
                            k.mm(pb_[:, hh * 128:(hh + 1) * 128], W1[:, kc, c0 + hh * 128:c0 + (hh + 1) * 128],
                                 hT[:, kc, :], kc == 0, kc == 7, [hT.d, W1d], [pb_.d])
                    return bi_, pb_

                gct = k.nxt("gct")
                cs_b = cosT[:, t, :, :].unsqueeze(2).broadcast_to([128, 2, 2, 16])
                sn_b = sinT[:, t, :, :].unsqueeze(2).broadcast_to([128, 2, 2, 16])
                gv4 = lambda idx: gct[:, idx, :].rearrange("p (a f i) -> p a f i", a=2, f=2)
                k.tt('gpsimd', gv4(0), Gq[:, :, :, :], cs_b, ALU.mult, [Gq.d, cosT.d], [gct.d])
                k.tt('gpsimd', gv4(1), GqS[:, :, :, :], sn_b, ALU.mult, [GqS.d, sinT.d], [gct.d])
                k.tt('gpsimd', gv4(2), Gk[:, :, :, :], cs_b, ALU.mult, [Gk.d, cosT.d], [gct.d])
                k.tt('gpsimd', gv4(3), GkS[:, :, :, :], sn_b, ALU.mult, [GkS.d, sinT.d], [gct.d])

                def normrope(src_banks, nh, gi, dstrot, qn_, o_ss, o_rs, o_tmp):
                    w = nh * 64
                    c = 0
                    for (pb_, ncols) in src_banks:
                        k.act(sq[:, c:c + ncols], pb_[:, 0:ncols], AF.Square, [pb_.d], [sq.d])
                        c += ncols
                    k.red(sm[:, o_ss:o_ss + nh], sq[:, 0:w].rearrange("p (h d) -> p h d", d=64), [sq.d], [sm.d])
                    k.rstd(sm[:, o_rs:o_rs + nh], sm[:, o_ss:o_ss + nh], HD, [sm.d], [sm.d], sm[:, o_tmp:o_tmp + nh])
                    c = 0
                    for (pb_, ncols) in src_banks:
                        nhh = ncols // 64
                        h0 = c // 64
                        k.tt('vector', qn_[:, c:c + ncols].rearrange("p (h d) -> p h d", d=64),
                             pb_[:, 0:ncols].rearrange("p (h d) -> p h d", d=64),
                             sm[:, o_rs + h0:o_rs + h0 + nhh].unsqueeze(2).broadcast_to([128, nhh, 64]),
                             ALU.mult, [pb_.d, sm.d], [qn_.d])
                        c += ncols
                    q5 = qn_[:, 0:w].rearrange("p (h a f i) -> p h a f i", a=2, f=2, i=16)
                    o5 = dstrot[:, 0:w].rearrange("p (h a f i) -> p h a f i", a=2, f=2, i=16)
                    for f in range(2):
                        gc = gct[:, gi, :].rearrange("p (a f i) -> p a f i", a=2, f=2)[:, :, f, :]
                        gs = gct[:, gi + 1, :].rearrange("p (a f i) -> p a f i", a=2, f=2)[:, :, f, :]
                        t1v = t1[:, 0:nh * 32].rearrange("p (h a i) -> p h a i", a=2, i=16)
                        t2v = t2[:, 0:nh * 32].rearrange("p (h a i) -> p h a i", a=2, i=16)
                        k.tt('gpsimd', t1v, q5[:, :, :, f, :], gc.unsqueeze(1).broadcast_to([128, nh, 2, 16]), ALU.mult,
                             [qn_.d, gct.d], [t1.d])
                        k.tt('vector', t2v, q5[:, :, :, 1 - f, :], gs.unsqueeze(1).broadcast_to([128, nh, 2, 16]), ALU.mult,
                             [qn_.d, gct.d], [t2.d])
                        k.tt('vector', o5[:, :, :, f, :], t1v, t2v, ALU.add, [t1.d, t2.d], [dstrot.d])

                bg, pg = k.psum()
                for dr in range(2):
                    for kc in range(8):
                        k.mm(pg[0:16, dr * 128:(dr + 1) * 128], W1[:, kc, 3584 + dr * 16:3584 + (dr + 1) * 16], hT[:, kc, :],
                             kc == 0, kc == 7, [hT.d, W1d], [pg.d])
                gaTt = k.nxt("gaT")
                k.cp('vector', gaTt[0:16, :], pg[0:16, 0:256], [pg.d], [gaTt.d])
                k.free(bg)
                gp2 = k.nxt("gp2")
                for dr in range(2):
                    bx, px = k.psum()
                    k.mm(px[:, :], gaTt[0:17, dr * 128:(dr + 1) * 128], WD[0:17, dr, :], True, True, [gaTt.d, WD.d], [px.d])
                    k.act(gp2[:, dr, :], px[:, :], AF.Exp, [px.d], [gp2.d], scale=-1.0)
                    k.free(bx)
                    k.act(gp2[:, dr, :], gp2[:, dr, :], AF.Ln, [gp2.d, ones_f.d], [gp2.d], scale=1.0, bias=ones_f[:, 0:1])
                if t in tiles_out:
                    b0, p0 = proj_tok(0)
                    b1, p1 = proj_tok(512)
                    normrope([(p0, 512), (p1, 512)], 16, 0, qrot, qn, 0, 16, 32)
                    k.free(b0)
                    k.free(b1)
                E1 = k.nxt("E1")
                E2 = k.nxt("E2")
                for dr in range(2):
                    U = Uf if dr == 0 else Ub
                    bB, pB = k.psum()
                    for hh in range(4):
                        k.mm(pB[:, hh * 128:(hh + 1) * 128], gp2[:, dr, hh * 128:(hh + 1) * 128], U[:, :], True, True,
                             [gp2.d, U.d], [pB.d])
                    k.act(E1[:, dr, :], pB[:, :], AF.Exp, [pB.d], [E1.d], scale=-1.0 / 16)
                    k.act(E2[:, dr, :], pB[:, :], AF.Exp, [pB.d], [E2.d], scale=1.0 / 16)
                    lastc = 127 if dr == 0 else 0
                    k.act(DL[:, t, dr, :], pB[:, :].rearrange("p (h i) -> p h i", i=128)[:, :, lastc], AF.Exp,
                          [pB.d], [DLd[t]], scale=-1.0 / 16)
                    k.free(bB)
                b0, p0 = proj_tok(1024)
                normrope([(p0, 256)], 4, 2, krot, kn, 48, 52, 56)
                k.cp('scalar', VA[:, sl, :, :].rearrange("p g d -> p (g d)"), p0[:, 256:512], [p0.d], [VAd[sl]])
                k.free(b0)
                bi_, pb_ = k.psum()
                pv = pb_.ap.bitcast(BF16)
                for g in range(4):
                    k.tr(pv[0:64, g * 128:(g + 1) * 128], krot[:, g * 64:(g + 1) * 64], ident[:, :], [krot.d, ident.d], [pb_.d])
                k.cp('vector', KT[0:64, :, sl * 128:(sl + 1) * 128],
                     pv[0:64, 0:512].rearrange("p (g s) -> p g s", s=128), [pb_.d], [KTd[sl]])
                k.free(bi_)
                if t == 0 or t == NT - 1:
                    for o_ in ([0] if t == 0 else []) + ([768] if t == NT - 1 else []):
                        k.cp('gpsimd', hb[0:64, o_:o_ + 512].rearrange("p (g s) -> p g s", s=128),
                             KT[0:64, :, sl * 128:(sl + 1) * 128], [KTd[sl]], [hb.d])
                        k.cp('gpsimd', hb[:, o_ + 512:o_ + 768], VA[:, sl, :, :].rearrange("p g d -> p (g d)"), [VAd[sl]], [hb.d])
                ER = k.nxt("ER")
                for dr in range(2):
                    SU = SUf if dr == 0 else SUb
                    bR, pR = k.psum()
                    k.mm(pR[:, :], SU[:, :], gp2[:, dr, :], True, True, [gp2.d, SU.d], [pR.d])
                    k.act(ER[:, dr, :], pR[:, :], AF.Exp, [pR.d], [ER.d], scale=-1.0 / 16)
                    k.free(bR)
                bq, pq = proj_feat(1536)
                qin = k.nxt("qin")
                for dr in range(2):
                    k.stt(qin[:, dr, :], pq[:, :], float(128 ** -0.5), E1[:, dr, :], ALU.mult, ALU.mult, [pq.d, E1.d], [qin.d])
                k.free(bq)
                bk_, pk = proj_feat(2048)
                kin = k.nxt("kin")
                k.tt('vector', kin[:, 0, :], pk[:, :], E2[:, 0, :], ALU.mult, [pk.d, E2.d], [kin.d])
                k.tt('vector', kin[:, 1, :], pk[:, :], E2[:, 1, :], ALU.mult, [pk.d, E2.d], [kin.d])
                k.free(bk_)
                bk_, pk = proj_tok(2048)
                kout = k.nxt("kout")
                k.tt('vector', kout[:, 0, :], pk[:, :], ER[:, 0, :], ALU.mult, [pk.d, ER.d], [kout.d])
                k.tt('vector', kout[:, 1, :], pk[:, :], ER[:, 1, :], ALU.mult, [pk.d, ER.d], [kout.d])
                k.free(bk_)
                if t in tiles_out:
                    for dr in range(2):
                        k.dma('gpsimd', SC_qin[t, dr], qin[:, dr, :], [qin.d], [dd('qin', t, dr)])
                        k.dma('gpsimd', SC_kin[t, dr], kin[:, dr, :], [kin.d], [dd('kin', t, dr)])
                for dr in range(2):
                    k.dma('gpsimd', SC_kout[t, dr], kout[:, dr, :], [kout.d], [dd('kout', t, dr)])
                gv = k.nxt("gv")
                for half in range(2):
                    b_, p_ = proj_tok(2560 + half * 512)
                    k.cp('scalar', gv[:, half * 512:(half + 1) * 512], p_[:, :], [p_.d], [gv.d])
                    k.free(b_)
                k.dma('gpsimd', SC_gv[t], gv[:, :], [gv.d], [dd('gv', t)])
                if t in tiles_out:
                    for half in range(2):
                        bi_, pb_ = k.psum()
                        pv = pb_.ap.bitcast(BF16)
                        for hh in range(8):
                            hd_ = half * 8 + hh
                            k.tr(pv[0:64, hh * 128:(hh + 1) * 128], qrot[:, hd_ * 64:(hd_ + 1) * 64], ident[:, :],
                                 [qrot.d, ident.d], [pb_.d])
                        k.cp('scalar', qT[0:64, half * 1024:(half + 1) * 1024], pv[0:64, 0:1024], [pb_.d], [qT.d])
                        k.free(bi_)
                    k.dma('gpsimd', SC_qT[t], qT[0:64, :], [qT.d], [dd('qT', t)])
                for dr in range(2):
                    if dr == 0:
                        acc = Scf if isctx else Af
                    else:
                        acc = Scb if isctx else Ab
                    pcol = (2 if isctx else 0) + dr
                    for hp in range(2):
                        bS, pS = k.psum()
                        for h2 in range(2):
                            hh = hp * 2 + h2
                            k.mm(pS[:, h2 * 256:(h2 + 1) * 256], kout[:, dr, hh * 128:(hh + 1) * 128],
                                 gv[:, hh * 256:(hh + 1) * 256], True, True, [kout.d, gv.d], [pS.d])
                        for h2 in range(2):
                            hh = hp * 2 + h2
                            if dr == 0:
                                k.stt(acc[:, hh, :], acc[:, hh, :], DL[:, t, 0, hh:hh + 1], pS[:, h2 * 256:(h2 + 1) * 256],
                                      ALU.mult, ALU.add, [acc.d, DLd[t], pS.d], [acc.d])
                            else:
                                k.stt(acc[:, hh, :], pS[:, h2 * 256:(h2 + 1) * 256], PP[:, pcol, hh:hh + 1], acc[:, hh, :],
                                      ALU.mult, ALU.add, [acc.d, PP.d, pS.d], [acc.d])
                        k.free(bS)
                    k.tt('vector', PP[:, pcol, :], PP[:, pcol, :], DL[:, t, dr, :], ALU.mult, [PP.d, DLd[t]], [PP.d])

            dXS, dXG = Dep('XS'), Dep('XG')
            xs = XS[l].ap()
            k.dma('sync', xs[:, 0:1024], Af[:, :, :].rearrange("p h e -> p (h e)"), [Af.d], [dXS])
            k.dma('sync', xs[:, 1024:2048], Ab[:, :, :].rearrange("p h e -> p (h e)"), [Ab.d], [dXS])
            k.dma('sync', xs[:, 2048:2056], PP[:, 0:2, :].rearrange("p a h -> p (a h)"), [PP.d], [dXS])
            k.dma('sync', xs[:, 2056:FS].bitcast(BF16), hb[:, :], [hb.d], [dXS])
            if NC == 1:
                k.dma('sync', XG[l].ap()[:, :], XS[l].ap()[:, :], [dXS], [dXG])
            else:
                fw.coll(lambda e, l=l: e.collective_compute("AllGather", ALU.bypass,
                                                            replica_groups=[list(range(NC))],
                                                            ins=[XS[l].ap().opt()], outs=[XG[l].ap().opt()]),
                        [dXS], [dXG])
            fw.barrier()
            k.release_to(mA)

            mA = k.mark()
            W2 = k.buf("W2", [8, 3072], BF16)
            W2d = W2.d
            for kc in range(8):
                load_w(W2[:, kc, 0:1024], wv[:, kc, 3584:4608], W2d)
                load_w(W2[:, kc, 1024:3072], wv[:, kc, 4640:6688], W2d)
            k.rot("hT2", 2, [8, 128], BF16)
            k.rot("go", 3, [D], BF16)
            for t in tiles_out:
                hT = k.nxt("hT2")
                k.dma('sync', hT[:, :, :].rearrange("p a b -> p (a b)"), SC_hT[t], [dd('hT', t)], [hT.d])
                for gi_, (dst, fn, key) in enumerate(((SC_gr, AF.Silu, 'gr'), (SC_ga, AF.Sigmoid, 'ga'),
                                                      (SC_gb, AF.Sigmoid, 'gb'))):
                    go = k.nxt("go")
                    for half in range(2):
                        bi_, pb_ = k.psum()
                        c0 = gi_ * 1024 + half * 512
                        for kc in range(8):
                            k.mm(pb_[:, :], hT[:, kc, :], W2[:, kc, c0:c0 + 512], kc == 0, kc == 7, [hT.d, W2d], [pb_.d])
                        k.act(go[:, half * 512:(half + 1) * 512], pb_[:, :], fn, [pb_.d], [go.d])
                        k.free(bi_)
                    k.dma('gpsimd', dst[t], go[:, :], [go.d], [dd(key, t)])
            fw.barrier()
            k.release_to(mA)

            mB = k.mark()
            GG = k.buf("GG", [256], F32)
            k.dma('sync', GG[:, :], gla_gain[l:l + 1, :].partition_broadcast(128), [], [GG.d])
            esk = k.buf("esk", [NH], F32)
            k.dma('sync', esk[0:64, :], sink[l:l + 1, :].partition_broadcast(64), [], [esk.d])
            k.act(esk[0:64, :], esk[0:64, :], AF.Exp, [esk.d], [esk.d])
            hacc = k.buf("hacc", [1536], F32)
            k.rot("hin", 2, [1536], BF16)
            k.memset('gpsimd', hacc[:, :], 0.0, [hacc.d])
            xg = XG[l].ap()
            for c in range(NC):
                hi = k.nxt("hin")
                k.dma('sync', hi[:, :], xg[c * 128:(c + 1) * 128, 2056:FS].bitcast(BF16), [dXG], [hi.d])
                k.stt(hacc[:, 768:1536], hi[:, 768:1536], metab[:, c_sl + c:c_sl + c + 1], hacc[:, 768:1536],
                      ALU.mult, ALU.add, [hi.d, metab.d, hacc.d], [hacc.d])
                k.stt(hacc[:, 0:768], hi[:, 0:768], metab[:, c_sr + c:c_sr + c + 1], hacc[:, 0:768],
                      ALU.mult, ALU.add, [hi.d, metab.d, hacc.d], [hacc.d])
            k.cp('vector', KT[0:64, :, 0:128], hacc[0:64, 768:1280].rearrange("p (g s) -> p g s", s=128), [hacc.d], [KTd[0]])
            k.cp('vector', VA[:, 0, :, :].rearrange("p g d -> p (g d)"), hacc[:, 1280:1536], [hacc.d], [VAd[0]])
            k.cp('vector', KT[0:64, :, (NT + 1) * 128:(NT + 2) * 128], hacc[0:64, 0:512].rearrange("p (g s) -> p g s", s=128),
                 [hacc.d], [KTd[NT + 1]])
            k.cp('vector', VA[:, NT + 1, :, :].rearrange("p g d -> p (g d)"), hacc[:, 512:768], [hacc.d], [VAd[NT + 1]])
            Sf, Sbk = Scf, Scb
            Sfb = k.buf("Sfb", [4, 256], BF16)
            sbt = k.buf("sbt", [4 * 256], BF16)
            k.rot("ain", 2, [1032], F32)
            k.rot("pe", 2, [4], F32)
            for (S_, aoff, poff, mcol, order) in ((Sf, 0, 2048, c_mf, range(NC)), (Sbk, 1024, 2052, c_mb, range(NC - 1, -1, -1))):
                for c in order:
                    a_ = k.nxt("ain")
                    p_ = k.nxt("pe")
                    k.dma('sync', a_[:, 0:1024], xg[c * 128:(c + 1) * 128, aoff:aoff + 1024], [dXG], [a_.d])
                    k.dma('sync', a_[:, 1024:1028], xg[c * 128:(c + 1) * 128, poff:poff + 4], [dXG], [a_.d])
                    mc = metab[:, mcol + c:mcol + c + 1]
                    k.ts('vector', p_[:, :], a_[:, 1024:1028], -1.0, ALU.add, [a_.d, metab.d], [p_.d], s2=mc, op1=ALU.mult)
                    k.ts('vector', p_[:, :], p_[:, :], 1.0, ALU.add, [p_.d], [p_.d])
                    for hh in range(4):
                        k.ts('vector', S_[:, hh, :], S_[:, hh, :], p_[:, hh:hh + 1], ALU.mult, [S_.d, p_.d], [S_.d])
                        k.stt(S_[:, hh, :], a_[:, hh * 256:(hh + 1) * 256], mc, S_[:, hh, :], ALU.mult, ALU.add,
                              [a_.d, metab.d, S_.d], [S_.d])
            k.rot("kob", 2, [512], BF16)
            k.rot("gvb", 2, [D], BF16)
            k.rot("qTb", 2, [2048], BF16)
            k.rot("qinb", 2, [2, 512], BF16)
            k.rot("kinb", 2, [2, 512], BF16)
            k.rot("kof", 2, [512], BF16)
            k.rot("grb", 2, [D], BF16)
            k.rot("sbl", 2, [D], BF16)
            k.rot("pT", 5, [512], BF16)
            k.rot("attnT", 2, [NH, 128], BF16)
            dn = k.buf("dn", [512], F32)
            aTm = k.buf("aTm", [2, 512], BF16)
            glaf = k.buf("glaf", [D], F32)
            k.rot("gla", 2, [D], BF16)
            k.rot("ofp", 2, [D], F32)
            k.rot("glaT", 2, [8, 128], BF16)
            k.rot("sm2", 2, [16], F32)
            sqo = k.buf("sqo", [D], F32)

            def bwd_pass(tl, S_):
                def put(t_):
                    k.cp('scalar', sbt[:, :], S_[:, :, :].rearrange("p h e -> p (h e)"), [S_.d], [sbt.d])
                    k.dma('gpsimd', SC_SB[t_], sbt[:, :], [sbt.d], [dd('SB', t_)])
                put(tl[-1])
                for idx in range(len(tl) - 1, 0, -1):
                    t = tl[idx]
                    ko = k.nxt("kob")
                    gvb = k.nxt("gvb")
                    k.dma('sync', ko[:, :], SC_kout[t, 1], [dd('kout', t, 1)], [ko.d])
                    k.dma('sync', gvb[:, :], SC_gv[t], [dd('gv', t)], [gvb.d])
                    for hp in range(2):
                        bS, pS = k.psum()
                        for h2 in range(2):
                            hh = hp * 2 + h2
                            k.mm(pS[:, h2 * 256:(h2 + 1) * 256], ko[:, hh * 128:(hh + 1) * 128],
                                 gvb[:, hh * 256:(hh + 1) * 256], True, True, [ko.d, gvb.d], [pS.d])
                        for h2 in range(2):
                            hh = hp * 2 + h2
                            k.stt(S_[:, hh, :], S_[:, hh, :], DL[:, t, 1, hh:hh + 1], pS[:, h2 * 256:(h2 + 1) * 256],
                                  ALU.mult, ALU.add, [S_.d, DLd[t], pS.d], [S_.d])
                        k.free(bS)
                    put(tl[idx - 1])

            bwd_pass(list(range(NT)), Sbk)
            if not last:
                k.memset('gpsimd', Sbk[:, :, :], 0.0, [Sbk.d])
                bwd_pass([NT, NT + 1], Sbk)

            pending = [None]
            segs = [(list(range(NT)), False)]
            if not last:
                segs.append(([NT, NT + 1], True))
            for (tl, zero) in segs:
                if zero:
                    k.memset('gpsimd', Sf[:, :, :], 0.0, [Sf.d])
                k.cp('scalar', Sfb[:, :, :], Sf[:, :, :], [Sf.d], [Sfb.d])
                for t in tl:
                    isctx = t >= NT
                    sl = slot(t)
                    qTb = k.nxt("qTb")
                    qinb = k.nxt("qinb")
                    kinb = k.nxt("kinb")
                    kof = k.nxt("kof")
                    gvb = k.nxt("gvb")
                    grb = k.nxt("grb")
                    sbl = k.nxt("sbl")
                    k.dma('sync', qTb[0:64, :], SC_qT[t], [dd('qT', t)], [qTb.d])
                    for dr in range(2):
                        k.dma('sync', qinb[:, dr, :], SC_qin[t, dr], [dd('qin', t, dr)], [qinb.d])
                        k.dma('sync', kinb[:, dr, :], SC_kin[t, dr], [dd('kin', t, dr)], [kinb.d])
                    k.dma('sync', kof[:, :], SC_kout[t, 0], [dd('kout', t, 0)], [kof.d])
                    k.dma('sync', gvb[:, :], SC_gv[t], [dd('gv', t)], [gvb.d])
                    k.dma('sync', grb[:, :], SC_gr[t], [dd('gr', t)], [grb.d])
                    k.dma('sync', sbl[:, :], SC_SB[t], [dd('SB', t)], [sbl.d])
                    if isctx:
                        kbl = [(NT + 2, None), (NT + 3, None)]
                    else:
                        kbl = [(sl - 1, maskL if t == 0 else triL), (sl, None), (sl + 1, maskR if t == NT - 1 else triR),
                               (NT + 2, None), (NT + 3, None)]
                    attnT = k.nxt("attnT")
                    items = [(g, bi2, ks, msk) for g in range(4) for bi2, (ks, msk) in enumerate(kbl)]
                    nit = len(items)
                    LA = 3
                    pTs = {}
                    grp = {}
                    for it in range(nit + LA):
                        if it < nit:
                            g, bi2, ks, msk = items[it]
                            bs_, ps_ = k.psum()
                            k.mm(ps_[:, :], KT[0:64, g, ks * 128:(ks + 1) * 128], qTb[0:64, g * 512:(g + 1) * 512],
                                 True, True, [KTd[ks], qTb.d], [ps_.d])
                            pT = k.nxt("pT")
                            k.act(pT[:, :], ps_[:, :], AF.Exp, [ps_.d], [pT.d])
                            k.free(bs_)
                            if msk is not None:
                                k.tt('gpsimd', pT[:, :].rearrange("p (h q) -> p h q", q=128),
                                     pT[:, :].rearrange("p (h q) -> p h q", q=128),
                                     msk[:, :].unsqueeze(1).broadcast_to([128, 4, 128]), ALU.mult, [pT.d, msk.d], [pT.d])
                            pTs[it] = pT
                        jt = it - LA
                        if jt >= 0:
                            g, bi2, ks, msk = items[jt]
                            if bi2 == 0:
                                grp[g] = (k.psum(), k.psum())
                            (bo, po), (bd, pd) = grp[g]
                            pT = pTs.pop(jt)
                            st_ = (bi2 == 0)
                            sp_ = (bi2 == len(kbl) - 1)
                            k.mm(po[0:64, :], VA[:, ks, g, :], pT[:, :], st_, sp_, [VAd[ks], pT.d], [po.d])
                            k.mm(pd[0:64, :], ones_b[:, :], pT[:, :], st_, sp_, [ones_b.d, pT.d], [pd.d])
                            if sp_:
                                for h4 in range(4):
                                    hh = g * 4 + h4
                                    k.ts('vector', dn[0:64, h4 * 128:(h4 + 1) * 128], pd[0:64, h4 * 128:(h4 + 1) * 128],
                                         esk[0:64, hh:hh + 1], ALU.add, [pd.d, esk.d], [dn.d])
                                k.free(bd)
                                k.recip(dn[0:64, :], dn[0:64, :], [dn.d], [dn.d])
                                k.tt('vector', attnT[0:64, g * 4:(g + 1) * 4, :].rearrange("p h q -> p (h q)"), po[0:64, :],
                                     dn[0:64, :], ALU.mult, [po.d, dn.d], [attnT.d])
                                k.free(bo)
                    k.dma('gpsimd', SC_aT[t], attnT[0:64, :, :].rearrange("p h q -> p (h q)"), [attnT.d], [dd('aT', t)])
                    for dr in range(2):
                        ba, pa = k.psum()
                        for hh in range(4):
                            k.mm(pa[:, hh * 128:(hh + 1) * 128], kinb[:, dr, hh * 128:(hh + 1) * 128],
                                 qinb[:, dr, hh * 128:(hh + 1) * 128], True, True, [kinb.d, qinb.d], [pa.d])
                        U = Uf if dr == 0 else Ub
                        k.tt('vector', aTm[:, dr, :].rearrange("p (h i) -> p h i", i=128),
                             pa[:, :].rearrange("p (h i) -> p h i", i=128),
                             U[:, :].unsqueeze(1).broadcast_to([128, 4, 128]), ALU.mult, [pa.d, U.d], [aTm.d])
                        k.free(ba)
                    sm2 = k.nxt("sm2")
                    ofp = k.nxt("ofp")
                    gla = k.nxt("gla")
                    for hp in range(2):
                        bo, po = k.psum()
                        for h2 in range(2):
                            hh = hp * 2 + h2
                            oc = po[:, h2 * 256:(h2 + 1) * 256]
                            vv = gvb[:, hh * 256:(hh + 1) * 256]
                            k.mm(oc, aTm[:, 0, hh * 128:(hh + 1) * 128], vv, True, False, [aTm.d, gvb.d], [po.d])
                            k.mm(oc, aTm[:, 1, hh * 128:(hh + 1) * 128], vv, False, False, [aTm.d, gvb.d], [po.d])
                            k.mm(oc, qinb[:, 0, hh * 128:(hh + 1) * 128], Sfb[:, hh, :], False, False, [qinb.d, Sfb.d], [po.d])
                            k.mm(oc, qinb[:, 1, hh * 128:(hh + 1) * 128], sbl[:, hh * 256:(hh + 1) * 256], False, True,
                                 [qinb.d, sbl.d], [po.d])
                        k.cp('scalar', ofp[:, hp * 512:(hp + 1) * 512], po[:, :], [po.d], [ofp.d])
                        k.free(bo)
                    for hp in range(2):
                        bS, pS = k.psum()
                        for h2 in range(2):
                            hh = hp * 2 + h2
                            k.mm(pS[:, h2 * 256:(h2 + 1) * 256], kof[:, hh * 128:(hh + 1) * 128],
                                 gvb[:, hh * 256:(hh + 1) * 256], True, True, [kof.d, gvb.d], [pS.d])
                        for h2 in range(2):
                            hh = hp * 2 + h2
                            k.stt(Sf[:, hh, :], Sf[:, hh, :], DL[:, t, 0, hh:hh + 1], pS[:, h2 * 256:(h2 + 1) * 256],
                                  ALU.mult, ALU.add, [Sf.d, DLd[t], pS.d], [Sf.d])
                        k.free(bS)
                    k.cp('scalar', Sfb[:, :, :], Sf[:, :, :], [Sf.d], [Sfb.d])
                    k.act(sqo[:, :], ofp[:, :], AF.Square, [ofp.d], [sqo.d])
                    k.red(sm2[:, 0:4], sqo[:, :].rearrange("p (h e) -> p h e", e=256), [sqo.d], [sm2.d])
                    k.rstd(sm2[:, 4:8], sm2[:, 0:4], 256, [sm2.d], [sm2.d], sm2[:, 8:12])
                    for hh in range(4):
                        k.stt(glaf[:, hh * 256:(hh + 1) * 256], ofp[:, hh * 256:(hh + 1) * 256], sm2[:, 4 + hh:5 + hh],
                              GG[:, :], ALU.mult, ALU.mult, [ofp.d, sm2.d, GG.d], [glaf.d])
                    k.tt('gpsimd', gla[:, :], glaf[:, :], grb[:, :], ALU.mult, [glaf.d, grb.d], [gla.d])

                    def finish(t_=t, gla_=gla):
                        bi_, pb_ = k.psum()
                        pv = pb_.ap.bitcast(BF16)
                        for kc in range(8):
                            k.tr(pv[:, kc * 128:(kc + 1) * 128], gla_[:, kc * 128:(kc + 1) * 128], ident[:, :],
                                 [gla_.d, ident.d], [pb_.d])
                        glaT = k.nxt("glaT")
                        k.cp('vector', glaT[:, :, :].rearrange("p a b -> p (a b)"), pv[:, 0:1024], [pb_.d], [glaT.d])
                        k.free(bi_)
                        k.dma('gpsimd', SC_gT[t_], glaT[:, :, :].rearrange("p a b -> p (a b)"), [glaT.d], [dd('gT', t_)])
                    if pending[0] is not None:
                        pending[0]()
                    pending[0] = finish
            if pending[0] is not None:
                pending[0]()
                pending[0] = None
            fw.barrier()
            k.release_to(mB)

            mB = k.mark()
            WA = k.buf("WA", [NH, D], BF16)
            WG = k.buf("WG", [8, D], BF16)
            WO = k.buf("WO", [8, D], BF16)
            Wd = Dep('Wmix')
            wav = w_ba[l].rearrange("(h d) n -> d h n", d=64)
            for hq in range(4):
                load_w(WA[0:64, hq * 4:(hq + 1) * 4, :], wav[:, hq * 4:(hq + 1) * 4, :], Wd)
            wgv = w_bg[l].rearrange("(kc p) n -> p kc n", p=128)
            wov = w_out[l].rearrange("(kc p) n -> p kc n", p=128)
            for kc in range(8):
                load_w(WG[:, kc, :], wgv[:, kc, :], Wd)
                load_w(WO[:, kc, :], wov[:, kc, :], Wd)
            GT1 = k.buf("GT1", [D], F32)
            k.rot("aTb", 2, [NH, 128], BF16)
            k.rot("gTb", 2, [8, 128], BF16)
            k.rot("gab", 2, [D], BF16)
            k.rot("gbb", 2, [D], BF16)
            k.rot("xr", 2, [D], F32)
            ysum = k.buf("ysum", [D], F32)
            ysum2 = k.buf("ysum2", [D], F32)
            y = k.buf("y", [D], BF16)
            yT = k.buf("yT", [8, 128], BF16)
            k.rot("xo", 2, [D], F32)
            for ti, t in enumerate(tiles_out):
                if ti == 0:
                    mod_bcast(GT1, l, 2, 0)
                elif t == NT:
                    mod_bcast(GT1, l, 2, 1)
                aTb = k.nxt("aTb")
                gTb = k.nxt("gTb")
                gab = k.nxt("gab")
                gbb = k.nxt("gbb")
                xr = k.nxt("xr")
                k.dma('sync', aTb[0:64, :, :].rearrange("p h q -> p (h q)"), SC_aT[t], [dd('aT', t)], [aTb.d])
                k.dma('sync', gTb[:, :, :].rearrange("p a b -> p (a b)"), SC_gT[t], [dd('gT', t)], [gTb.d])
                k.dma('sync', gab[:, :], SC_ga[t], [dd('ga', t)], [gab.d])
                k.dma('sync', gbb[:, :], SC_gb[t], [dd('gb', t)], [gbb.d])
                xs_ap, xs_d = xsrc(l, t)
                k.dma('sync', xr[:, :], xs_ap, [xs_d] if xs_d else [], [xr.d])
                for half in range(2):
                    cs = slice(half * 512, (half + 1) * 512)
                    bA, pA = k.psum()
                    for hh in range(NH):
                        k.mm(pA[:, :], aTb[0:64, hh, :], WA[0:64, hh, cs], hh == 0, hh == NH - 1, [aTb.d, Wd], [pA.d])
                    k.tt('vector', ysum[:, cs], pA[:, :], gab[:, cs], ALU.mult, [pA.d, gab.d], [ysum.d])
                    k.free(bA)
                    bG, pG = k.psum()
                    for kc in range(8):
                        k.mm(pG[:, :], gTb[:, kc, :], WG[:, kc, cs], kc == 0, kc == 7, [gTb.d, Wd], [pG.d])
                    k.tt('vector', ysum2[:, cs], pG[:, :], gbb[:, cs], ALU.mult, [pG.d, gbb.d], [ysum2.d])
                    k.free(bG)
                k.tt('gpsimd', y[:, :], ysum[:, :], ysum2[:, :], ALU.add, [ysum.d, ysum2.d], [y.d])
                bi_, pb_ = k.psum()
                pv = pb_.ap.bitcast(BF16)
                for kc in range(8):
                    k.tr(pv[:, kc * 128:(kc + 1) * 128], y[:, kc * 128:(kc + 1) * 128], ident[:, :], [y.d, ident.d], [pb_.d])
                k.cp('scalar', yT[:, :, :].rearrange("p a b -> p (a b)"), pv[:, 0:1024], [pb_.d], [yT.d])
                k.free(bi_)
                xo = k.nxt("xo")
                for half in range(2):
                    cs = slice(half * 512, (half + 1) * 512)
                    bO, pO = k.psum()
                    for kc in range(8):
                        k.mm(pO[:, :], yT[:, kc, :], WO[:, kc, cs], kc == 0, kc == 7, [yT.d, Wd], [pO.d])
                    k.tt('vector', xo[:, cs], pO[:, :], GT1[:, cs], ALU.mult, [pO.d, GT1.d], [xo.d])
                    k.free(bO)
                k.tt('gpsimd', xo[:, :], xo[:, :], xr[:, :], ALU.add, [xo.d, xr.d], [xo.d])
                k.dma('gpsimd', XM[t * 128:(t + 1) * 128, :], xo[:, :], [xo.d], [dd('XM', t)])
            fw.barrier()
            k.release_to(mL)

            mF = k.mark()
            WF1 = k.buf("WF1", [8, DFF], BF16)
            WF2 = k.buf("WF2", [32, D], BF16)
            Wfd = Dep('Wffn')
            w1v = w_ff1[l].rearrange("(kc p) n -> p kc n", p=128)
            w2v = w_ff2[l].rearrange("(kc p) n -> p kc n", p=128)
            for kc in range(8):
                load_w(WF1[:, kc, :], w1v[:, kc, :], Wfd)
            for kc in range(0, 32, 4):
                load_w(WF2[:, kc:kc + 4, :], w2v[:, kc:kc + 4, :], Wfd)
            G2 = k.buf("G2", [D], F32)
            SH2 = k.buf("SH2", [D], F32)
            GT2 = k.buf("GT2", [D], F32)
            ftmp = k.buf("ftmp", [D], F32)

            def load_mod2(j):
                k.dma('sync', ftmp[:, :], g_norm2[l:l + 1, :].partition_broadcast(128), [], [ftmp.d])
                mod_bcast(SH2, l, 3, j)
                mod_bcast(G2, l, 4, j)
                mod_bcast(GT2, l, 5, j)
                k.stt(G2[:, :], G2[:, :], 1.0, ftmp[:, :], ALU.add, ALU.mult, [G2.d, ftmp.d], [G2.d])
            k.rot("fx", 2, [D], F32)
            fsq = k.buf("fsq", [D], F32)
            fh = k.buf("fh", [D], BF16)
            fhT = k.buf("fhT", [8, 128], BF16)
            k.rot("fr", 2, [512], BF16)
            fuT = k.buf("fuT", [32, 128], BF16)
            k.rot("fo", 2, [D], F32)
            k.rot("fsm", 2, [4], F32)
            for ti, t in enumerate(tiles_out):
                if ti == 0:
                    load_mod2(0)
                elif t == NT:
                    load_mod2(1)
                fx = k.nxt("fx")
                fsm = k.nxt("fsm")
                k.dma('sync', fx[:, :], XM[t * 128:(t + 1) * 128, :], [dd('XM', t)], [fx.d])
                k.act(fsq[:, :], fx[:, :], AF.Square, [fx.d], [fsq.d])
                k.red(fsm[:, 0:1], fsq[:, :], [fsq.d], [fsm.d])
                k.rstd(fsm[:, 1:2], fsm[:, 0:1], D, [fsm.d], [fsm.d], fsm[:, 2:3])
                k.tt('gpsimd', ftmp[:, :], fx[:, :], G2[:, :], ALU.mult, [fx.d, G2.d], [ftmp.d])
                k.stt(fh[:, :], ftmp[:, :], fsm[:, 1:2], SH2[:, :], ALU.mult, ALU.add, [ftmp.d, fsm.d, SH2.d], [fh.d])
                bi_, pb_ = k.psum()
                pv = pb_.ap.bitcast(BF16)
                for kc in range(8):
                    k.tr(pv[:, kc * 128:(kc + 1) * 128], fh[:, kc * 128:(kc + 1) * 128], ident[:, :], [fh.d, ident.d], [pb_.d])
                k.cp('vector', fhT[:, :, :].rearrange("p a b -> p (a b)"), pv[:, 0:1024], [pb_.d], [fhT.d])
                k.free(bi_)
                for fb in range(8):
                    bU, pU = k.psum()
                    for f4 in range(4):
                        fc = fb * 4 + f4
                        for kc in range(8):
                            k.mm(pU[:, f4 * 128:(f4 + 1) * 128], WF1[:, kc, fc * 128:(fc + 1) * 128], fhT[:, kc, :],
                                 kc == 0, kc == 7, [fhT.d, Wfd], [pU.d])
                    fr = k.nxt("fr")
                    k.act(fr[:, :], pU[:, :], AF.Relu, [pU.d], [fr.d])
                    k.free(bU)
                    eng = 'gpsimd' if fb % 2 == 0 else 'vector'
                    k.tt(eng, fuT[:, fb * 4:(fb + 1) * 4, :].rearrange("p a b -> p (a b)"), fr[:, :], fr[:, :], ALU.mult,
                         [fr.d], [fuT.d])
                fo = k.nxt("fo")
                for half in range(2):
                    cs = slice(half * 512, (half + 1) * 512)
                    bO, pO = k.psum()
                    for fc in range(32):
                        k.mm(pO[:, :], fuT[:, fc, :], WF2[:, fc, cs], fc == 0, fc == 31, [fuT.d, Wfd], [pO.d])
                    k.tt('vector', fo[:, cs], pO[:, :], GT2[:, cs], ALU.mult, [pO.d, GT2.d], [fo.d])
                    k.free(bO)
                k.tt('gpsimd', fo[:, :], fo[:, :], fx[:, :], ALU.add, [fo.d, fx.d], [fo.d])
                if last:
                    k.dma('gpsimd', out[t * 128:(t + 1) * 128, :], fo[:, :], [fo.d], [dd('out', t)])
                else:
                    k.dma('gpsimd', XC[t * 128:(t + 1) * 128, :], fo[:, :], [fo.d], [dd('XC', t)])
            fw.barrier()
            k.release_to(mF)
        fw.barrier()
        fw.emit()
    return nc


def make_in_maps(inputs, NC, NT, DEPTH):
    f = lambda a: np.ascontiguousarray(np.asarray(a, dtype=np.float32))
    T = NT + 2
    WM = D // NC
    x = f(inputs["x"])[0]
    ctx = f(inputs["ctx"])[0]
    cvec = np.stack([f(inputs["c"])[0], f(inputs["c_ctx"])], 0)
    cvecT = np.ascontiguousarray(cvec.reshape(2, 8, 128).transpose(2, 1, 0))
    w_mod = f(inputs["w_mod"])
    b_mod = f(inputs["b_mod"])
    shared = {k_: f(inputs[k_]) for k_ in ("g_norm1", "g_norm2", "w_in", "q_gain", "k_gain", "sink", "w_decay",
                                           "b_decay", "gla_gain", "w_branch_attn", "w_branch_gla", "w_out",
                                           "w_ff1", "w_ff2")}
    shared["ctx"] = ctx
    shared["cvecT"] = cvecT
    maps = []
    for c in range(NC):
        m = dict(shared)
        m["x"] = np.ascontiguousarray(x[c * NT * 128:(c + 1) * NT * 128])
        wm = w_mod.reshape(DEPTH, D, 6, NC, WM)[:, :, :, c, :].reshape(DEPTH, D, 6 * WM)
        m["w_mod_s"] = np.ascontiguousarray(wm)
        bm = b_mod.reshape(DEPTH, 6, NC, WM)[:, :, c, :].reshape(1, DEPTH * 6 * WM)
        m["b_mod_s"] = np.ascontiguousarray(bm)
        meta = np.zeros((128, 2 * T + 4 * NC + 2), np.float32)
        p = np.arange(128)
        for t in range(NT):
            tok = c * NT * 128 + t * 128 + p
            meta[:, 2 * t] = tok // 64
            meta[:, 2 * t + 1] = tok % 64
        o = 2 * T
        for c2 in range(NC):
            meta[:, o + c2] = 1.0 if c2 < c else 0.0
            meta[:, o + NC + c2] = 1.0 if c2 > c else 0.0
            meta[:, o + 2 * NC + c2] = 1.0 if c2 == c - 1 else 0.0
            meta[:, o + 3 * NC + c2] = 1.0 if c2 == c + 1 else 0.0
        meta[:, o + 4 * NC] = 1.0 if c > 0 else 0.0
        meta[:, o + 4 * NC + 1] = 1.0 if c < NC - 1 else 0.0
        m["meta"] = meta
        maps.append(m)
    return maps


_CACHE = {}


def kernel(**inputs):
    NC, NT, DEPTH = 8, 16, 4
    key = (NC, NT, DEPTH)
    if key not in _CACHE:
        _CACHE[key] = build_program(NC, NT, DEPTH)
    nc = _CACHE[key]
    maps = make_in_maps(inputs, NC, NT, DEPTH)
    res = run_bass_kernel_spmd(nc, maps, core_ids=list(range(NC)))
    outs = [np.asarray(res.results[c]["out"], dtype=np.float32) for c in range(NC)]
    return np.concatenate(outs, 0)[None]
```
